# Optimizing a Trainium2 kernel written in Bass

```python
import jax, jax.numpy as jnp
from jax import lax
import numpy as np

D_MODEL = 1024
BATCH = 4
SEQ = 8192
DEPTH = 1
DEC_BATCH = 16
DEC_SEQ = 64
PAST_LEN = 1024

CHUNK = 64
POOL_WINDOWS = (2, 4, 8, 16)
N_POOL_GROUPS = len(POOL_WINDOWS)
POOL_WIDTH = D_MODEL // 2
POOL_GROUP_DIM = POOL_WIDTH // N_POOL_GROUPS
POOL_HIST = max(POOL_WINDOWS) - 1
GMLP_CHUNK = 128
N_GMLP_HEADS = 4
GMLP_WIDTH = D_MODEL - POOL_WIDTH
GMLP_HEAD_DIM = GMLP_WIDTH // N_GMLP_HEADS
MIX_WIDTH = POOL_WIDTH + GMLP_WIDTH
IN_WIDTH = POOL_WIDTH + 2 * GMLP_WIDTH
D_FF = 4 * D_MODEL
EPS = 1e-6

kernel_name = "hymba_pool_gmlp_stream_step"


def _rmsnorm(x, g):
    xf = x.astype(jnp.float32)
    y = xf * lax.rsqrt(jnp.mean(xf * xf, axis=-1, keepdims=True) + EPS)
    return (y * g.astype(jnp.float32)).astype(x.dtype)


def _pool_mixer(a, hist, pos0, pool_w, pool_scale):
    B, T, _ = a.shape
    xc = jnp.concatenate([hist.astype(a.dtype), a], axis=1)
    cs = jnp.cumsum(xc.astype(jnp.float32), axis=1)
    cs0 = jnp.concatenate([jnp.zeros((B, 1, POOL_WIDTH), jnp.float32), cs], axis=1)
    pos = pos0 + jnp.arange(T)
    means = []
    for g, w in enumerate(POOL_WINDOWS):
        sl = slice(g * POOL_GROUP_DIM, (g + 1) * POOL_GROUP_DIM)
        s = cs0[:, POOL_HIST + 1:, sl] - cs0[:, POOL_HIST + 1 - w:POOL_HIST + 1 - w + T, sl]
        cnt = jnp.minimum(pos + 1, w).astype(jnp.float32)
        means.append(s / cnt[None, :, None])
    m = jnp.concatenate(means, axis=-1)
    d = (m - a.astype(jnp.float32)).astype(a.dtype)
    d = d.reshape(B, T, N_POOL_GROUPS, POOL_GROUP_DIM)
    out = jnp.einsum('btgc,gce->btge', d, pool_w).reshape(B, T, POOL_WIDTH) * pool_scale
    new_hist = xc[:, -POOL_HIST:]
    return out, new_hist


def _gmlp_gate(u, v, gmlp_ws, gmlp_b):
    B, T, _ = v.shape
    L = min(T, GMLP_CHUNK)
    idx = jnp.arange(L)
    mask = (idx[None, :] // CHUNK) <= (idx[:, None] // CHUNK)
    wsm = jnp.where(mask[None], gmlp_ws[:, :L, :L], jnp.zeros((), gmlp_ws.dtype))
    vr = v.reshape(B, T // L, L, N_GMLP_HEADS, GMLP_HEAD_DIM)
    mixed = jnp.einsum('gij,bnjgd->bnigd', wsm, vr)
    mixed = mixed + gmlp_b[:, :L].T[None, None, :, :, None]
    return u * mixed.reshape(B, T, GMLP_WIDTH)


def _layer(x, pool_hist, pos0, norm1_g, w_in, pool_w, pool_scale, gmlp_ws, gmlp_b,
           w_out, norm2_g, w_up, w_down):
    h = _rmsnorm(x, norm1_g)
    z = h @ w_in
    a = z[..., :POOL_WIDTH]
    u = jax.nn.gelu(z[..., POOL_WIDTH:POOL_WIDTH + GMLP_WIDTH])
    v = jax.nn.gelu(z[..., POOL_WIDTH + GMLP_WIDTH:])
    pool_out, new_hist = _pool_mixer(a, pool_hist, pos0, pool_w, pool_scale)
    gmlp_out = _gmlp_gate(u, v, gmlp_ws, gmlp_b)
    x = x + jnp.concatenate([pool_out, gmlp_out], axis=-1) @ w_out
    f = _rmsnorm(x, norm2_g) @ w_up
    x = x + jnp.square(jax.nn.relu(f)) @ w_down
    return x, new_hist, v


def setup_inputs(seed: int = 0) -> dict:
    key = jax.random.key(seed)
    ks = jax.random.split(key, 16)
    f32 = jnp.float32
    nrm = lambda k, s, sc: jax.random.normal(k, s, f32) * sc
    return {
        "x_prompt": nrm(ks[0], (BATCH, SEQ, D_MODEL), 1.0),
        "x_sample": nrm(ks[1], (DEC_BATCH, DEC_SEQ, D_MODEL), 1.0),
        "state_pool": nrm(ks[2], (DEPTH, DEC_BATCH, POOL_HIST, POOL_WIDTH), 1.0),
        "norm1_g": 1.0 + nrm(ks[3], (DEPTH, D_MODEL), 0.02),
        "w_in": nrm(ks[4], (DEPTH, D_MODEL, IN_WIDTH), D_MODEL ** -0.5),
        "pool_w": nrm(ks[5], (DEPTH, N_POOL_GROUPS, POOL_GROUP_DIM, POOL_GROUP_DIM), POOL_GROUP_DIM ** -0.5),
        "pool_scale": 1.0 + nrm(ks[6], (DEPTH, POOL_WIDTH), 0.1),
        "gmlp_ws": nrm(ks[7], (DEPTH, N_GMLP_HEADS, GMLP_CHUNK, GMLP_CHUNK), GMLP_CHUNK ** -0.5),
        "gmlp_b": 1.0 + nrm(ks[8], (DEPTH, N_GMLP_HEADS, GMLP_CHUNK), 0.1),
        "w_out": nrm(ks[9], (DEPTH, MIX_WIDTH, D_MODEL), MIX_WIDTH ** -0.5),
        "norm2_g": 1.0 + nrm(ks[10], (DEPTH, D_MODEL), 0.02),
        "w_up": nrm(ks[11], (DEPTH, D_MODEL, D_FF), D_MODEL ** -0.5),
        "w_down": nrm(ks[12], (DEPTH, D_FF, D_MODEL), D_FF ** -0.5),
        "normf_g": 1.0 + nrm(ks[13], (D_MODEL,), 0.02),
    }


def reference(x_prompt, x_sample, state_pool, norm1_g, w_in, pool_w, pool_scale, gmlp_ws,
              gmlp_b, w_out, norm2_g, w_up, w_down, normf_g):
    hp = x_prompt
    hs = x_sample
    pool_p, pool_s, v_s = [], [], []
    for l in range(DEPTH):
        zero_hist = jnp.zeros((hp.shape[0], POOL_HIST, POOL_WIDTH), hp.dtype)
        hp, nhp, _ = _layer(hp, zero_hist, 0, norm1_g[l], w_in[l], pool_w[l], pool_scale[l],
                            gmlp_ws[l], gmlp_b[l], w_out[l], norm2_g[l], w_up[l], w_down[l])
        hs, nhs, vs = _layer(hs, state_pool[l], PAST_LEN, norm1_g[l], w_in[l], pool_w[l],
                             pool_scale[l], gmlp_ws[l], gmlp_b[l], w_out[l], norm2_g[l],
                             w_up[l], w_down[l])
        pool_p.append(nhp)
        pool_s.append(nhs)
        v_s.append(vs)
    y_prompt = _rmsnorm(hp, normf_g)
    y_sample = _rmsnorm(hs, normf_g)
    state_pool_prompt = jnp.stack(pool_p, axis=0)
    state_pool_sample = jnp.stack(pool_s, axis=0)
    state_gmlp_v_sample = jnp.stack(v_s, axis=0)
    return (y_prompt, y_sample, state_pool_prompt, state_pool_sample, state_gmlp_v_sample)
```

```python
import contextlib
import numpy as np
import concourse.bass as bass
import concourse.mybir as mybir
from concourse.bass_utils import run_bass_kernel_spmd

F32 = mybir.dt.float32
BF16 = mybir.dt.bfloat16
AF = mybir.ActivationFunctionType
ALU = mybir.AluOpType

NCORES = 8
D = 1024
NT = 3
NS = 11
TOK = NT * 128
NTILES = NT * NS
AW = 432
NRING = 5
NXB = 3
EPS = 1e-6
POOL_W = (2, 4, 8, 16)

ENGS = ("pe", "act", "dve", "pool", "sp")


class Chan:
    def __init__(self, name, inc):
        self.name, self.inc, self.count, self.sem = name, inc, 0, None


class Ev:
    __slots__ = ("chan", "val")

    def __init__(self, chan, val):
        self.chan, self.val = chan, val


class Prog:
    def __init__(self):
        self.ops = {e: [] for e in ENGS}
        self.chans = {}
        self.lastw = {}
        self.readers = {}
        self.engchan = {e: self.chan("c_" + e, 1) for e in ENGS}

    def chan(self, name, inc=16):
        if name not in self.chans:
            self.chans[name] = Chan(name, inc)
        return self.chans[name]

    def add(self, eng, fn, reads=(), writes=(), dma=None):
        waits = []
        for r in reads:
            w = self.lastw.get(r)
            if w is not None:
                waits.append(w)
        for wr in writes:
            w = self.lastw.get(wr)
            if w is not None:
                waits.append(w)
            waits.extend(self.readers.get(wr, ()))
        ch = self.engchan[eng] if dma is None else self.chan(dma, 16)
        ch.count += ch.inc
        ev = Ev(ch, ch.count)
        for r in reads:
            self.readers.setdefault(r, []).append(ev)
        for wr in writes:
            self.lastw[wr] = ev
            self.readers[wr] = []
        self.ops[eng].append((fn, waits, ch))
        return ev

    def emit(self, nc, es, final_waits=()):
        for ch in self.chans.values():
            ch.sem = es.enter_context(nc.semaphore(ch.name))
        block = es.enter_context(nc.Block())
        prog = self

        def run(engname, e, final=False):
            waited = {}
            pe_ch = prog.engchan["pe"]
            for fn, waits, chn in prog.ops[engname]:
                need = {}
                for w in waits:
                    if engname == "pe" and w.chan is pe_ch:
                        continue
                    if w.val > need.get(w.chan, 0):
                        need[w.chan] = w.val
                for ch, v in need.items():
                    if waited.get(ch, 0) >= v:
                        continue
                    e.wait_ge(ch.sem, v)
                    waited[ch] = v
                ins = fn(e)
                ins.then_inc(chn.sem, chn.inc)
            if final:
                for ev in final_waits:
                    e.wait_ge(ev.chan.sem, ev.val)

        @block.tensor
        def _(e):
            run("pe", e)

        @block.scalar
        def _(e):
            run("act", e)

        @block.vector
        def _(e):
            run("dve", e)

        @block.gpsimd
        def _(e):
            run("pool", e)

        @block.sync
        def _(e):
            run("sp", e, final=True)


def build_nc():
    nc = bass.Bass("TRN2", target_bir_lowering=False)

    def din(name, shape, dt=F32):
        return nc.dram_tensor(name, list(shape), dt, kind="ExternalInput").ap()

    def dout(name, shape, dt=F32):
        return nc.dram_tensor(name, list(shape), dt, kind="ExternalOutput").ap()

    xall = din("xall", [NTILES * 128, D])
    xh = din("xh", [128, D])
    hist = din("hist", [2, 15, 512])
    w_in = din("w_in", [D, 1536])
    w_out = din("w_out", [D, D])
    w_up = din("w_up", [D, 4096])
    w_down = din("w_down", [4096, D])
    pool_w = din("pool_w", [4, 128, 128])
    pool_scale = din("pool_scale", [512])
    gmlp_ws = din("gmlp_ws", [4, 128, 128])
    gmlp_b = din("gmlp_b", [4, 128])
    g1 = din("norm1_g", [D])
    g2 = din("norm2_g", [D])
    gf = din("normf_g", [D])
    ident = din("ident", [128, 128])
    invc = din("invc", [128, 64])
    yall = dout("yall", [NTILES * 128, D])
    spp = dout("spp", [15, 512])
    sps = dout("sps", [2, 15, 512])
    vs = dout("vs", [128, 512])
    scr = nc.dram_tensor("scr", [16, 128, 8, 512], BF16, kind="Internal").ap()

    P = Prog()
    with contextlib.ExitStack() as es:
        def sb(name, shape, dt):
            return es.enter_context(nc.sbuf_tensor(name, list(shape), dt))

        X = [sb(f"X{b}", [128, NT, D], F32) for b in range(NXB)]
        XH = X[1][:, 0, :]
        WSF = X[1][:, 1, 0:512].rearrange("p (g j) -> p g j", j=128)
        WSFS = X[1][:, 1, 512:1024].rearrange("p (g j) -> p g j", j=128)
        HB = [sb(f"HB{j}", [128, D], BF16) for j in range(3)]
        HT = sb("HT", [128, 8, TOK], BF16)
        H2T = sb("H2T", [128, 8, TOK], BF16)
        A = sb("A", [128, 4, AW], F32)
        PQ = [sb(f"PQ{j}", [128, AW], F32) for j in range(2)]
        T16 = sb("T16", [128, 16], F32)
        DT = sb("DT", [128, 4, TOK], BF16)
        UT = sb("UT", [128, 4, TOK], BF16)
        V = sb("V", [128, NT, 512], BF16)
        MIXT = sb("MIXT", [128, 8, TOK], BF16)
        ST = sb("ST", [128, 32, TOK], BF16)
        HTH = ST[:, 0:3, :].rearrange("p a b -> p (a b)")[:, 0:1024].rearrange("p (k t) -> p k t", t=128)
        WSB = ST[:, 3:5, :].rearrange("p a b -> p (a b)")[:, 0:512].rearrange("p (g t) -> p g t", t=128)
        WSBS = ST[:, 5:7, :].rearrange("p a b -> p (a b)")[:, 0:512].rearrange("p (g t) -> p g t", t=128)
        VF = X[2][:, 0, 512:1024]
        WIN = sb("WIN", [128, 8, 1536], BF16)
        WOUT = sb("WOUT", [128, 8, D], BF16)
        RING = [sb(f"RING{s}", [128, 8, 512], BF16) for s in range(NRING)]
        G1B = sb("G1B", [128, D], F32)
        G2B = sb("G2B", [128, D], F32)
        GFB = sb("GFB", [128, D], F32)
        JUNK = sb("JUNK", [128, D], BF16)
        IDF = sb("IDF", [128, 128], F32)
        IDB = sb("IDB", [128, 128], BF16)
        POOLW = sb("POOLW", [128, 4, 128], BF16)
        PSC = sb("PSC", [128, 4], F32)
        INVC = sb("INVC", [128, 64], F32)
        WSMT = sb("WSMT", [128, 4, 128], BF16)
        WSMTS = sb("WSMTS", [128, 4, 128], BF16)
        BB = sb("BB", [128, 4, 128], F32)
        BBS = sb("BBS", [128, 4, 128], F32)
        GT = [sb(f"GT{j}", [128, TOK], F32) for j in range(1)]
        HSA = sb("HSA", [128, 4, 2, 16], F32)
        assert (NS - 1) % NXB != 2 and (NS - 2) % NXB != 2
        SPO = X[2][0:15, :, 0:512]
        SS = {k: sb("SS" + k, [128, 4], F32) for k in ("1", "2", "f", "h")}
        RS = {k: sb("RS" + k, [128, 4], F32) for k in ("1", "2", "f", "h")}
        VAR = {k: sb("VAR" + k, [128, 4], F32) for k in ("1", "2", "f", "h")}
        NWT = {k: sb("NWT" + k, [128, 4], F32) for k in ("1", "2", "f", "h")}
        ps = es.enter_context(nc.psum_tensor("ps", [128, 8, 512], F32))

        def bank(i):
            return ps[:, i, :]

        def bankbf(i):
            return ps[:, i, :].bitcast(BF16)

        front_banks = [5, 6, 7]
        up_banks = [3, 4, 0, 1, 2]
        st = {"front": 0, "up": 0, "hb": 0, "ring": 0, "gt": 0}

        def next_front():
            b = front_banks[st["front"] % len(front_banks)]
            st["front"] += 1
            return b

        def next_up():
            b = up_banks[st["up"] % len(up_banks)]
            st["up"] += 1
            return b

        def next_hb():
            j = st["hb"] % 3
            st["hb"] += 1
            return j

        add = P.add

        add("dve", lambda e: e.memset(WSFS, 0.0), writes=["WSFS"])
        add("dve", lambda e: e.memset(A[:], 0.0), writes=[("A", g) for g in range(4)] + [("Ah", g) for g in range(4)])
        add("dve", lambda e: e.memset(HSA[:], 0.0), writes=["HSA"])
        add("sp", lambda e: e.dma_start(out=IDF[:], in_=ident[:, :]), writes=["IDF"], dma="su0")
        for n in range(NT):
            add("sp", lambda e, n=n: e.dma_start(out=X[0][:, n, :], in_=xall[n * 128:(n + 1) * 128, :]),
                writes=[("X", 0, n)], dma=f"xld0sp_{n}")
        add("sp", lambda e: e.dma_start(out=XH, in_=xh[:, :]), writes=[("X", 1, 0)], dma="su2")
        add("sp", lambda e: e.dma_start(out=G1B[:], in_=g1.partition_broadcast(128)), writes=["G1B"], dma="su1")
        add("sp", lambda e: e.dma_start(out=WSF, in_=gmlp_ws.rearrange("g i j -> i g j")), writes=[("X", 1, 1)], dma="su8")
        add("sp", lambda e: e.dma_start(out=WSFS[0:64, :, 0:64], in_=gmlp_ws[:, 0:64, 0:64].rearrange("g i j -> i g j")),
            writes=["WSFS"], dma="su9")
        add("sp", lambda e: e.dma_start(out=WSFS[64:128, :, 64:128], in_=gmlp_ws[:, 0:64, 0:64].rearrange("g i j -> i g j")),
            reads=["WSFS"], writes=["WSFSb"], dma="su10")
        add("sp", lambda e: e.dma_start(out=BB[:].rearrange("p g c -> p (g c)"),
                                        in_=gmlp_b.rearrange("g c -> (g c)").partition_broadcast(128)),
            writes=["BB"], dma="sb0")
        for half in range(2):
            add("sp", lambda e, half=half: e.dma_start(out=BBS[:, :, half * 64:(half + 1) * 64],
                                                       in_=gmlp_b[:, 0:64].partition_broadcast(128)),
                writes=[("BBS", half)], dma=f"sb1_{half}")
        add("sp", lambda e: e.dma_start(out=PSC[:], in_=pool_scale.rearrange("(g p) -> p g", p=128),
                                        allow_slow_non_contiguous=True), writes=["PSC"], dma="su5")
        add("sp", lambda e: e.dma_start(out=INVC[:], in_=invc[:, :]), writes=["INVC"], dma="su6")
        add("sp", lambda e: e.dma_start(out=G2B[:], in_=g2.partition_broadcast(128)), writes=["G2B"], dma="su3")
        add("sp", lambda e: e.dma_start(out=GFB[:], in_=gf.partition_broadcast(128)), writes=["GFB"], dma="su4")

        def gen_L(i):
            b = i % NXB
            for n in range(NT):
                t = i * NT + n
                add("pool", lambda e, b=b, n=n, t=t: e.dma_start(out=X[b][:, n, :], in_=xall[t * 128:(t + 1) * 128, :]),
                    writes=[("X", b, n)], dma=f"xld{b}_{n}")

        for j in range(3):
            add("pool", lambda e, j=j: e.dma_start(
                out=WIN[:, :, j * 512:(j + 1) * 512],
                in_=w_in[:, j * 512:(j + 1) * 512].rearrange("(k p) c -> p k c", p=128)),
                writes=[("WIN", j)], dma=f"win{j}")
        for j in range(2):
            add("pool", lambda e, j=j: e.dma_start(
                out=WOUT[:, :, j * 512:(j + 1) * 512],
                in_=w_out[:, j * 512:(j + 1) * 512].rearrange("(k p) c -> p k c", p=128)),
                writes=[("WOUT", j)], dma=f"wout{j}")
        add("pool", lambda e: e.dma_start(out=POOLW[:], in_=pool_w.rearrange("g c e -> c g e")),
            writes=["POOLW"], dma="su7")
        def gen_setup_gmlp_T():
            for (src, dst, nm) in ((WSB, WSMT, "WSB"), (WSBS, WSMTS, "WSBS")):
                fb = next_front()

                def trw(e, src=src, fb=fb):
                    for g in range(4):
                        i_ = e.transpose(out=bankbf(fb)[:, g * 128:(g + 1) * 128], in_=src[:, g, :], identity=IDB[:])
                    return i_
                add("pe", trw, reads=[nm, "IDB"], writes=[("BANK", fb)])
                add("act", lambda e, dst=dst, fb=fb: e.activation(
                    out=dst[:], in_=bankbf(fb)[:, 0:512].rearrange("p (g t) -> p g t", t=128), func=AF.Copy),
                    writes=[("BANK", fb), "WSMT" + nm])

        def gen_setup_gmlp():
            add("dve", lambda e: e.tensor_copy(out=WSB, in_=WSF), reads=[("X", 1, 1)], writes=["WSB"])
            add("dve", lambda e: e.memset(WSB[0:64, :, 64:128], 0.0), writes=["WSB"])
            add("dve", lambda e: e.tensor_copy(out=WSBS, in_=WSFS), reads=["WSFS", "WSFSb", ("X", 1, 1)], writes=["WSBS"])

        for s in range(2):
            for g in range(4):
                add("sp", lambda e, s=s, g=g: e.dma_start(
                    out=HSA[:, g, s, 1:16], in_=hist[s, :, g * 128:(g + 1) * 128].rearrange("r c -> c r"),
                    allow_slow_non_contiguous=True),
                    reads=["HSA"], writes=[("HSAw", s, g)], dma=f"hs{s}{g}")
        for k in ("1", "2", "f", "h"):
            add("dve", lambda e, k=k: e.memset(SS[k][:], 0.0), writes=[("SS", k)])

        def norm_memset(kind):
            add("dve", lambda e: e.memset(SS[kind][:], 0.0), writes=[("SS", kind)])

        def norm_square(kind, j, ap, res):
            add("act", lambda e, ap=ap, j=j: e.activation(out=JUNK[:], in_=ap, func=AF.Square,
                                                         accum_out=SS[kind][:, j:j + 1]),
                reads=list(res) + [("SS", kind)], writes=[("SSc", kind, j), "JUNK"])

        def norm_stats(kind, srcs, gb, squares_done=False):
            n = len(srcs)
            if not squares_done:
                norm_memset(kind)
                for j, (ap, res) in enumerate(srcs):
                    norm_square(kind, j, ap, res)
            VR, NW = VAR[kind], NWT[kind]
            add("dve", lambda e: e.tensor_scalar(out=VR[:, 0:n], in0=SS[kind][:, 0:n], scalar1=1.0 / D, scalar2=EPS,
                                                 op0=ALU.mult, op1=ALU.add),
                reads=[("SSc", kind, j) for j in range(n)] + [("SS", kind)], writes=[("VAR", kind)])
            add("act", lambda e: e.activation(out=RS[kind][:, 0:n], in_=VR[:, 0:n], func=AF.Sqrt),
                reads=[("VAR", kind)], writes=[("RS", kind)])
            add("dve", lambda e: e.reciprocal(out=RS[kind][:, 0:n], in_=RS[kind][:, 0:n]), writes=[("RS", kind)])
            add("dve", lambda e: e.tensor_tensor(out=NW[:, 0:n], in0=RS[kind][:, 0:n], in1=RS[kind][:, 0:n], op=ALU.mult),
                reads=[("RS", kind)], writes=[("NWT", kind)])
            add("dve", lambda e: e.tensor_tensor(out=NW[:, 0:n], in0=NW[:, 0:n], in1=VR[:, 0:n], op=ALU.mult),
                reads=[("VAR", kind)], writes=[("NWT", kind)])
            add("dve", lambda e: e.tensor_scalar(out=NW[:, 0:n], in0=NW[:, 0:n], scalar1=-0.5, scalar2=1.5,
                                                 op0=ALU.mult, op1=ALU.add), writes=[("NWT", kind)])
            add("dve", lambda e: e.tensor_tensor(out=RS[kind][:, 0:n], in0=RS[kind][:, 0:n], in1=NW[:, 0:n], op=ALU.mult),
                reads=[("NWT", kind)], writes=[("RS", kind)])

        def norm_apply(kind, j, ap, res, gb, gbres):
            hb = next_hb()
            add("dve", lambda e: e.scalar_tensor_tensor(out=HB[hb][:], in0=ap, scalar=RS[kind][:, j:j + 1], in1=gb[:],
                                                        op0=ALU.mult, op1=ALU.mult),
                reads=list(res) + [("RS", kind), gbres], writes=[("HB", hb)])
            return hb

        def norm_T(hb, dstT, dst_res, ncol0):
            fb = next_front()

            def tr(e):
                for k in range(8):
                    i_ = e.transpose(out=bankbf(fb)[:, k * 128:(k + 1) * 128], in_=HB[hb][:, k * 128:(k + 1) * 128],
                                     identity=IDB[:])
                return i_
            add("pe", tr, reads=[("HB", hb), "IDB"], writes=[("BANK", fb)])
            add("act", lambda e: e.activation(out=dstT[:, :, ncol0:ncol0 + 128],
                                              in_=bankbf(fb)[:, 0:1024].rearrange("p (k t) -> p k t", t=128),
                                              func=AF.Copy),
                writes=[("BANK", fb), dst_res])

        def norm_apply_T(kind, j, ap, res, gb, gbres, dstT, dst_res, ncol0):
            norm_T(norm_apply(kind, j, ap, res, gb, gbres), dstT, dst_res, ncol0)

        WINres = [("WIN", 0), ("WIN", 1), ("WIN", 2)]
        WOUTres = [("WOUT", 0), ("WOUT", 1)]

        def segs(i):
            if i == NS - 1:
                return [(16, 0, 256), (288, 256, 64), (368, 320, 64)]
            return [(16, 0, TOK)]

        def gen_halo():
            srcs = [(X[0][:, n, :], [("X", 0, n)]) for n in range(NT)] + [(XH, [("X", 1, 0)])]
            norm_stats("1", srcs, G1B)
            add("dve", lambda e: e.tensor_copy(out=IDB[:], in_=IDF[:]), reads=["IDF"], writes=["IDB"])
            norm_apply_T("1", NT, XH, [("X", 1, 0)], G1B, "G1B", HTH, "HTH", 0)
            fb = next_front()

            def mm(e):
                for m in range(4):
                    for k in range(8):
                        i_ = e.matmul(bank(fb)[:, m * 128:(m + 1) * 128], lhsT=WIN[:, k, m * 128:(m + 1) * 128],
                                      rhs=HTH[:, k, :], start=(k == 0), stop=(k == 7))
                return i_
            add("pe", mm, reads=["HTH", ("WIN", 0)], writes=[("BANK", fb)])
            add("act", lambda e: e.activation(out=A[:, :, 0:16],
                                              in_=bank(fb)[:, :].rearrange("p (m t) -> p m t", t=128)[:, :, 112:128],
                                              func=AF.Copy),
                writes=[("BANK", fb)] + [("Ah", g) for g in range(4)])

        hbs = {}

        def gen_N1_elem(i, stats=True):
            b = i % NXB
            srcs = [(X[b][:, n, :], [("X", b, n)]) for n in range(NT)]
            if stats:
                norm_stats("1", srcs, G1B)
            hbs[("1", i)] = [norm_apply("1", n, X[b][:, n, :], [("X", b, n)], G1B, "G1B") for n in range(NT)]

        def gen_N1_T(i):
            for n in range(NT):
                norm_T(hbs[("1", i)][n], HT, ("HT", n), n * 128)

        def gen_WI(i):
            sg = segs(i)
            HTall = [("HT", n) for n in range(NT)]
            for m in range(8):
                fb = next_front()

                def mm(e, m=m, fb=fb):
                    for k in range(8):
                        i_ = e.matmul(bank(fb)[:, 0:TOK], lhsT=WIN[:, k, m * 128:(m + 1) * 128], rhs=HT[:, k, :],
                                      start=(k == 0), stop=(k == 7))
                    return i_
                add("pe", mm, reads=HTall + [("WIN", m // 4)], writes=[("BANK", fb)])
                if m < 4:
                    for si, (acol, tok0, ln) in enumerate(sg):
                        add("act", lambda e, m=m, fb=fb, acol=acol, tok0=tok0, ln=ln: e.activation(
                            out=A[:, m, acol:acol + ln], in_=bank(fb)[:, tok0:tok0 + ln], func=AF.Copy),
                            writes=[("BANK", fb), ("A", m)])
                else:
                    add("act", lambda e, m=m, fb=fb: e.activation(out=UT[:, m - 4, :], in_=bank(fb)[:, 0:TOK],
                                                                  func=AF.Gelu_apprx_tanh),
                        writes=[("BANK", fb), ("UT", m - 4)])
            for n in range(NT):
                fb = next_front()
                is_sample = (i * NT + n == NTILES - 1)

                def mm(e, n=n, fb=fb):
                    for k in range(8):
                        i_ = e.matmul(bank(fb)[:, :], lhsT=HT[:, k, n * 128:(n + 1) * 128], rhs=WIN[:, k, 1024:1536],
                                      start=(k == 0), stop=(k == 7))
                    return i_
                add("pe", mm, reads=[("HT", n), ("WIN", 2)], writes=[("BANK", fb)])
                if not is_sample:
                    add("act", lambda e, n=n, fb=fb: e.activation(out=V[:, n, :], in_=bank(fb)[:, :],
                                                                  func=AF.Gelu_apprx_tanh),
                        writes=[("BANK", fb), ("V", n)])
                else:
                    add("act", lambda e, n=n, fb=fb: e.activation(out=VF, in_=bank(fb)[:, :],
                                                                  func=AF.Gelu_apprx_tanh),
                        writes=[("BANK", fb), ("X", 2, 0)])
                    add("dve", lambda e, n=n: e.tensor_copy(out=V[:, n, :], in_=VF), reads=[("X", 2, 0)], writes=[("V", n)])
                    fin.append(add("pool", lambda e: e.dma_start(out=vs[:, :], in_=VF), reads=[("X", 2, 0)], dma="o_vs"))
            if i == NS - 1:
                for si, c0 in enumerate((241, 305, 369)):
                    fb = next_front()

                    def mm(e, c0=c0, fb=fb):
                        for k in range(8):
                            i_ = e.matmul(bank(fb)[0:15, :], lhsT=HT[:, k, c0:c0 + 15], rhs=WIN[:, k, 0:512],
                                          start=(k == 0), stop=(k == 7))
                        return i_
                    add("pe", mm, reads=HTall + [("WIN", 0)], writes=[("BANK", fb)])
                    add("act", lambda e, si=si, fb=fb: e.activation(out=SPO[:, si, :], in_=bank(fb)[0:15, :], func=AF.Copy),
                        writes=[("BANK", fb), ("X", 2, si)])
                fin.append(add("pool", lambda e: e.dma_start(out=spp[:, :], in_=SPO[:, 0, :]), reads=[("X", 2, 0)], dma="o_spp"))
                fin.append(add("pool", lambda e: e.dma_start(out=sps.rearrange("s r c -> r s c"), in_=SPO[:, 1:3, :]),
                               reads=[("X", 2, 1), ("X", 2, 2)], dma="o_sps"))

        def gen_PL(i):
            sg = segs(i)
            W = AW if i == NS - 1 else 16 + TOK
            if i == NS - 1:
                for s in range(2):
                    c0 = 272 + 80 * s
                    add("dve", lambda e, s=s, c0=c0: e.tensor_copy(out=A[:, :, c0:c0 + 16], in_=HSA[:, :, s, :]),
                        reads=["HSA"] + [("HSAw", s, g) for g in range(4)], writes=[("A", g) for g in range(4)])
            for g in range(4):
                src = A[:, g, :]
                src_res = [("A", g), ("Ah", g)]
                cur = None
                sh = 1
                for lvl in range(g + 1):
                    dstj = lvl % 2
                    dst = PQ[dstj]
                    lo = 2 * sh - 1
                    if lvl == 0:
                        i0, i1 = src[:, lo:W], src[:, lo - sh:W - sh]
                        rd = src_res
                    else:
                        i0, i1 = cur[:, lo:W], cur[:, lo - sh:W - sh]
                        rd = [("PQ", 1 - dstj)]
                    add("dve", lambda e, dst=dst, lo=lo, i0=i0, i1=i1: e.tensor_tensor(out=dst[:, lo:W], in0=i0, in1=i1,
                                                                                    op=ALU.add),
                        reads=rd, writes=[("PQ", dstj)])
                    cur = dst
                    curj = dstj
                    sh *= 2
                wg = POOL_W[g]
                for (acol, tok0, ln) in sg:
                    add("dve", lambda e, g=g, cur=cur, acol=acol, tok0=tok0, ln=ln, wg=wg: e.scalar_tensor_tensor(
                        out=DT[:, g, tok0:tok0 + ln], in0=cur[:, acol:acol + ln], scalar=1.0 / wg,
                        in1=A[:, g, acol:acol + ln], op0=ALU.mult, op1=ALU.subtract),
                        reads=[("PQ", curj), ("A", g)], writes=[("DT", g)])
                if i == 0:
                    add("dve", lambda e, g=g, cur=cur: e.tensor_tensor(out=T16[:], in0=cur[:, 16:32],
                                                                       in1=INVC[:, g * 16:(g + 1) * 16], op=ALU.mult),
                        reads=[("PQ", curj), "INVC"], writes=["T16"])
                    add("dve", lambda e, g=g: e.tensor_tensor(out=DT[:, g, 0:16], in0=T16[:], in1=A[:, g, 16:32],
                                                              op=ALU.subtract),
                        reads=["T16", ("A", g)], writes=[("DT", g)])
            if i < NS - 1:
                add("dve", lambda e: e.tensor_copy(out=A[:, :, 0:16], in_=A[:, :, TOK:TOK + 16]),
                    reads=[("A", g) for g in range(4)], writes=[("Ah", g) for g in range(4)])

        def gen_PG_pieces(i):
            pieces = []

            def pool_piece(g):
                fb = next_front()
                add("pe", lambda e, g=g, fb=fb: e.matmul(bank(fb)[:, 0:TOK], lhsT=POOLW[:, g, :], rhs=DT[:, g, :],
                                                        start=True, stop=True),
                    reads=["POOLW", ("DT", g)], writes=[("BANK", fb)])
                add("dve", lambda e, g=g, fb=fb: e.tensor_scalar(out=MIXT[:, g, :], in0=bank(fb)[:, 0:TOK],
                                                                scalar1=PSC[:, g:g + 1], scalar2=None, op0=ALU.mult),
                    reads=["PSC"], writes=[("BANK", fb), ("MIXT", g)])

            def gmlp_piece(g):
                fb = next_front()

                def mm(e, g=g, fb=fb):
                    for n in range(NT):
                        smp = (i * NT + n == NTILES - 1)
                        wm = WSMTS if smp else WSMT
                        i_ = e.matmul(bank(fb)[:, n * 128:(n + 1) * 128], lhsT=V[:, n, g * 128:(g + 1) * 128], rhs=wm[:, g, :],
                                      start=True, stop=True)
                    return i_
                add("pe", mm, reads=[("V", n) for n in range(NT)] + ["WSMTWSB", "WSMTWSBS"], writes=[("BANK", fb)])
                gj = st["gt"] % len(GT)
                st["gt"] += 1
                npr = NT - 1 if i == NS - 1 else NT
                add("dve", lambda e, g=g, fb=fb, gj=gj, npr=npr: e.tensor_tensor(
                    out=GT[gj][:, 0:npr * 128].rearrange("p (n c) -> p n c", c=128),
                    in0=bank(fb)[:, 0:npr * 128].rearrange("p (n c) -> p n c", c=128),
                    in1=BB[:, g, :].unsqueeze(1).to_broadcast([128, npr, 128]), op=ALU.add),
                    reads=["BB"], writes=[("BANK", fb), ("GT", gj)])
                if npr < NT:
                    add("dve", lambda e, g=g, fb=fb, gj=gj, npr=npr: e.tensor_tensor(
                        out=GT[gj][:, npr * 128:TOK], in0=bank(fb)[:, npr * 128:TOK], in1=BBS[:, g, :], op=ALU.add),
                        reads=[("BBS", 0), ("BBS", 1)], writes=[("BANK", fb), ("GT", gj)])
                add("dve", lambda e, g=g, gj=gj: e.tensor_tensor(out=MIXT[:, 4 + g, :], in0=GT[gj][:, :], in1=UT[:, g, :],
                                                                op=ALU.mult),
                    reads=[("UT", g), ("GT", gj)], writes=[("MIXT", 4 + g)])

            for g in range(4):
                pieces.append(lambda g=g: pool_piece(g))
            for g in range(4):
                pieces.append(lambda g=g: gmlp_piece(g))
            return pieces

        def gen_PG(i):
            for p in gen_PG_pieces(i):
                p()

        def gen_WO_pe(i):
            b = i % NXB
            MIXall = [("MIXT", k) for k in range(8)]
            for n in range(NT):
                for hh in range(2):
                    fb = next_front()

                    def mm(e, n=n, hh=hh, fb=fb):
                        for k in range(8):
                            i_ = e.matmul(bank(fb)[:, :], lhsT=MIXT[:, k, n * 128:(n + 1) * 128],
                                          rhs=WOUT[:, k, hh * 512:(hh + 1) * 512], start=(k == 0), stop=(k == 7))
                        return i_
                    add("pe", mm, reads=MIXall + [("WOUT", hh)], writes=[("BANK", fb)])
                    add("dve", lambda e, n=n, hh=hh, fb=fb: e.tensor_tensor(
                        out=X[b][:, n, hh * 512:(hh + 1) * 512], in0=bank(fb)[:, :], in1=X[b][:, n, hh * 512:(hh + 1) * 512],
                        op=ALU.add),
                        writes=[("BANK", fb), ("X", b, n)])

        def gen_N2_elem(i):
            b = i % NXB
            srcs = [(X[b][:, n, :], [("X", b, n)]) for n in range(NT)]
            norm_stats("2", srcs, G2B)
            hbs[("2", i)] = [norm_apply("2", n, X[b][:, n, :], [("X", b, n)], G2B, "G2B") for n in range(NT)]

        def gen_N2_T(i):
            for n in range(NT):
                norm_T(hbs[("2", i)][n], H2T, ("H2T", n), n * 128)

        pending_stores = []

        def flush_stores():
            while pending_stores:
                s_, q_ = pending_stores.pop(0)
                add("sp", lambda e, s_=s_, q_=q_: e.dma_start(out=scr[q_], in_=RING[s_][:]),
                    reads=[("RING", s_)], writes=[("SCR", q_)], dma=f"scrst{s_}")

        def ring_load(i, q):
            seq = i * 16 + q
            s = seq % NRING
            late = (q % 2 == 1)
            cast = (i == 0) or (i == 1 and late)
            store = (i == 0 and not late) or (i == 1 and late)
            if cast:
                if q < 8:
                    src = w_up[:, q * 512:(q + 1) * 512].rearrange("(k p) c -> p k c", p=128)
                else:
                    hh, kg = divmod(q - 8, 4)
                    src = w_down[kg * 1024:(kg + 1) * 1024, hh * 512:(hh + 1) * 512].rearrange("(k p) c -> p k c", p=128)
                add("pool", lambda e, s=s, src=src: e.dma_start(out=RING[s][:], in_=src),
                    writes=[("RING", s)], dma=f"ringc{s}")
                flush_stores()
                if store:
                    pending_stores.append((s, q))
            else:
                flush_stores()
                add("sp", lambda e, s=s, q=q: e.dma_start(out=RING[s][:], in_=scr[q]),
                    reads=[("SCR", q)], writes=[("RING", s)], dma=f"ring{s}")

        def ring_next(i, q, ahead=NRING - 1):
            seq = i * 16 + q + ahead
            i2, q2 = divmod(seq, 16)
            if i2 < NS:
                ring_load(i2, q2)

        def gen_UP(i, m0, m1):
            H2all = [("H2T", n) for n in range(NT)]
            for m in range(m0, m1):
                q = m // 4
                s = (i * 16 + q) % NRING
                ub = (m + 3) % 5

                def mm(e, m=m, s=s, ub=ub):
                    for k in range(8):
                        i_ = e.matmul(bank(ub)[:, 0:TOK], lhsT=RING[s][:, k, (m % 4) * 128:(m % 4 + 1) * 128],
                                      rhs=H2T[:, k, :], start=(k == 0), stop=(k == 7))
                    return i_
                add("pe", mm, reads=H2all + [("RING", s)], writes=[("BANK", ub)])
                add("act", lambda e, m=m, ub=ub: e.activation(out=ST[:, m, :], in_=bank(ub)[:, 0:TOK], func=AF.Relu),
                    writes=[("BANK", ub), ("ST", m)])
                add("dve", lambda e, m=m: e.tensor_tensor(out=ST[:, m, :], in0=ST[:, m, :], in1=ST[:, m, :], op=ALU.mult),
                    writes=[("ST", m)])
                if m % 4 == 3:
                    ring_next(i, q)

        def gen_DN(i, hh, kg, tile_final=False):
            b = i % NXB
            q = 8 + hh * 4 + kg
            s = (i * 16 + q) % NRING
            for n in range(NT):
                def mm(e, n=n, s=s):
                    for kk in range(8):
                        i_ = e.matmul(bank(n)[:, :], lhsT=ST[:, kg * 8 + kk, n * 128:(n + 1) * 128], rhs=RING[s][:, kk, :],
                                      start=(kg == 0 and kk == 0), stop=(kg == 3 and kk == 7))
                    return i_
                add("pe", mm, reads=[("ST", kg * 8 + kk) for kk in range(8)] + [("RING", s)], writes=[("BANK", n)])
                if kg == 3:
                    add("dve", lambda e, n=n: e.tensor_tensor(
                        out=X[b][:, n, hh * 512:(hh + 1) * 512], in0=bank(n)[:, :], in1=X[b][:, n, hh * 512:(hh + 1) * 512],
                        op=ALU.add),
                        writes=[("BANK", n), ("X", b, n)])
                    if tile_final:
                        norm_square("f", n, X[b][:, n, :], [("X", b, n)])
            ring_next(i, q)

        def gen_final(i, tiles=None, squares_done=False):
            b = i % NXB
            tiles = list(range(NT)) if tiles is None else tiles
            srcs = [(X[b][:, n, :], [("X", b, n)]) for n in tiles]
            norm_stats("f", srcs, GFB, squares_done=squares_done)
            for j, n in enumerate(tiles):
                t = i * NT + n
                add("dve", lambda e, n=n, j=j: e.scalar_tensor_tensor(out=X[b][:, n, :], in0=X[b][:, n, :],
                                                                      scalar=RS["f"][:, j:j + 1], in1=GFB[:],
                                                                      op0=ALU.mult, op1=ALU.mult),
                    reads=[("RS", "f"), "GFB"], writes=[("X", b, n)])
                lastst = (i == NS - 1)
                ev = add("sp" if lastst else "pool",
                         lambda e, n=n, t=t: e.dma_start(out=yall[t * 128:(t + 1) * 128, :], in_=X[b][:, n, :]),
                         reads=[("X", b, n)], dma=(f"ystsp_{n}" if lastst else f"yst{b}_{n}"))
                fin_y[(b, n)] = ev

        fin = []
        fin_y = {}
        gen_halo()
        gen_N1_elem(0, stats=False)
        gen_N1_T(0)
        gen_setup_gmlp()
        gen_WI(0)
        gen_setup_gmlp_T()
        gen_L(1)
        for q in range(NRING - 1):
            ring_load(0, q)
        gen_PL(0)
        gen_PG(0)
        gen_WO_pe(0)
        gen_N2_elem(0)
        gen_N2_T(0)
        gen_N1_elem(1)
        add("dve", lambda e: e.memset(T16[:], 0.0), reads=["HTH", "WSB", "WSBS"],
            writes=["T16"] + [("ST", m) for m in range(7)])
        for i in range(NS):
            nxt = i + 1 < NS
            last = i == NS - 1
            gen_UP(i, 0, 4)
            if i + 2 < NS:
                gen_L(i + 2)
            if nxt:
                gen_N1_T(i + 1)
            gen_UP(i, 4, 8)
            if nxt:
                gen_WI(i + 1)
                gen_PL(i + 1)
            if i == 0:
                gen_UP(i, 8, 24)
                if nxt:
                    gen_PG(i + 1)
                gen_UP(i, 24, 29)
                if nxt:
                    gen_WO_pe(i + 1)
                    gen_N2_elem(i + 1)
                gen_UP(i, 29, 32)
            else:
                gen_UP(i, 8, 18)
                pieces = gen_PG_pieces(i + 1) if nxt else []
                for k in range(8):
                    if pieces:
                        pieces[k]()
                    gen_UP(i, 18 + k, 19 + k)
                gen_UP(i, 26, 30)
                if nxt:
                    gen_WO_pe(i + 1)
                    gen_N2_elem(i + 1)
                gen_UP(i, 30, 32)
            gen_DN(i, 0, 0)
            gen_DN(i, 0, 1)
            gen_DN(i, 0, 2)
            if nxt:
                gen_N2_T(i + 1)
            gen_DN(i, 0, 3)
            gen_DN(i, 1, 0)
            if i + 2 < NS:
                gen_N1_elem(i + 2)
            gen_DN(i, 1, 1)
            gen_DN(i, 1, 2)
            if last:
                norm_memset("f")
            gen_DN(i, 1, 3, tile_final=last)
            gen_final(i, squares_done=last)
        fin.extend(fin_y.values())
        P.emit(nc, es, final_waits=fin)
    return nc


_NC_CACHE = {}


def kernel(x_prompt, x_sample, state_pool, norm1_g, w_in, pool_w, pool_scale, gmlp_ws, gmlp_b,
           w_out, norm2_g, w_up, w_down, normf_g):
    f32 = np.float32
    x_prompt = np.asarray(x_prompt, f32)
    x_sample = np.asarray(x_sample, f32)
    state_pool = np.asarray(state_pool, f32)
    B, S, _ = x_prompt.shape
    half = S // 2
    if "nc" not in _NC_CACHE:
        _NC_CACHE["nc"] = build_nc()
    nc = _NC_CACHE["nc"]
    shared = {
        "w_in": np.ascontiguousarray(np.asarray(w_in, f32)[0]),
        "w_out": np.ascontiguousarray(np.asarray(w_out, f32)[0]),
        "w_up": np.ascontiguousarray(np.asarray(w_up, f32)[0]),
        "w_down": np.ascontiguousarray(np.asarray(w_down, f32)[0]),
        "pool_w": np.ascontiguousarray(np.asarray(pool_w, f32)[0]),
        "pool_scale": np.ascontiguousarray(np.asarray(pool_scale, f32)[0]),
        "gmlp_ws": np.ascontiguousarray(np.asarray(gmlp_ws, f32)[0]),
        "gmlp_b": np.ascontiguousarray(np.asarray(gmlp_b, f32)[0]),
        "norm1_g": np.ascontiguousarray(np.asarray(norm1_g, f32)[0]),
        "norm2_g": np.ascontiguousarray(np.asarray(norm2_g, f32)[0]),
        "normf_g": np.ascontiguousarray(np.asarray(normf_g, f32)),
        "ident": np.eye(128, dtype=f32),
    }
    invc = np.zeros((2, 128, 64), f32)
    for g, w in enumerate(POOL_W):
        for t in range(16):
            invc[0, :, g * 16 + t] = 1.0 / min(t + 1, w)
            invc[1, :, g * 16 + t] = 1.0 / w
    in_maps = []
    for c in range(NCORES):
        b, hf = divmod(c, 2)
        xs = x_sample[2 * c:2 * c + 2].reshape(128, D)
        xall = np.concatenate([x_prompt[b, hf * half:(hf + 1) * half], xs], axis=0)
        if hf == 0:
            xh = np.zeros((128, D), f32)
        else:
            xh = x_prompt[b, half - 128:half]
        m = dict(shared)
        m["xall"] = np.ascontiguousarray(xall)
        m["xh"] = np.ascontiguousarray(xh)
        m["hist"] = np.ascontiguousarray(state_pool[0, 2 * c:2 * c + 2])
        m["invc"] = invc[hf]
        in_maps.append(m)
    res = run_bass_kernel_spmd(nc, in_maps, core_ids=list(range(NCORES)))
    rs = res.results
    y_prompt = np.empty((B, S, D), f32)
    y_sample = np.empty(x_sample.shape, f32)
    spp = np.empty((1, B, 15, 512), f32)
    spsm = np.empty((1, x_sample.shape[0], 15, 512), f32)
    vsm = np.empty((1, x_sample.shape[0], x_sample.shape[1], 512), f32)
    for c in range(NCORES):
        b, hf = divmod(c, 2)
        ya = rs[c]["yall"]
        y_prompt[b, hf * half:(hf + 1) * half] = ya[:half]
        y_sample[2 * c:2 * c + 2] = ya[half:].reshape(2, 64, D)
        if hf == 1:
            spp[0, b] = rs[c]["spp"]
        spsm[0, 2 * c:2 * c + 2] = rs[c]["sps"]
        vsm[0, 2 * c:2 * c + 2] = rs[c]["vs"].reshape(2, 64, 512)
    return (y_prompt, y_sample, spp, spsm, vsm)
```

```python
import contextlib
import numpy as np
import concourse.bass as bass
import concourse.mybir as mybir
from concourse.bass_utils import run_bass_kernel_spmd

F32 = mybir.dt.float32
BF16 = mybir.dt.bfloat16
AF = mybir.ActivationFunctionType
ALU = mybir.AluOpType

NCORES = 8
D = 1024
NT = 3
NS = 11
TOK = NT * 128
NTILES = NT * NS
AW = 432
NRING = 5
NXB = 3
EPS = 1e-6
POOL_W = (2, 4, 8, 16)

ENGS = ("pe", "act", "dve", "pool", "sp")


class Chan:
    def __init__(self, name, inc):
        self.name, self.inc, self.count, self.sem = name, inc, 0, None


class Ev:
    __slots__ = ("chan", "val")

    def __init__(self, chan, val):
        self.chan, self.val = chan, val


class Prog:
    def __init__(self):
        self.ops = {e: [] for e in ENGS}
        self.chans = {}
        self.lastw = {}
        self.readers = {}
        self.engchan = {e: self.chan("c_" + e, 1) for e in ENGS}

    def chan(self, name, inc=16):
        if name not in self.chans:
            self.chans[name] = Chan(name, inc)
        return self.chans[name]

    def add(self, eng, fn, reads=(), writes=(), dma=None):
        waits = []
        for r in reads:
            w = self.lastw.get(r)
            if w is not None:
                waits.append(w)
        for wr in writes:
            w = self.lastw.get(wr)
            if w is not None:
                waits.append(w)
            waits.extend(self.readers.get(wr, ()))
        ch = self.engchan[eng] if dma is None else self.chan(dma, 16)
        ch.count += ch.inc
        ev = Ev(ch, ch.count)
        for r in reads:
            self.readers.setdefault(r, []).append(ev)
        for wr in writes:
            self.lastw[wr] = ev
            self.readers[wr] = []
        self.ops[eng].append((fn, waits, ch))
        return ev

    def emit(self, nc, es, final_waits=()):
        for ch in self.chans.values():
            ch.sem = es.enter_context(nc.semaphore(ch.name))
        block = es.enter_context(nc.Block())
        prog = self

        def run(engname, e, final=False):
            waited = {}
            pe_ch = prog.engchan["pe"]
            for fn, waits, chn in prog.ops[engname]:
                need = {}
                for w in waits:
                    if engname == "pe" and w.chan is pe_ch:
                        continue
                    if w.val > need.get(w.chan, 0):
                        need[w.chan] = w.val
                for ch, v in need.items():
                    if waited.get(ch, 0) >= v:
                        continue
                    e.wait_ge(ch.sem, v)
                    waited[ch] = v
                ins = fn(e)
                ins.then_inc(chn.sem, chn.inc)
            if final:
                for ev in final_waits:
                    e.wait_ge(ev.chan.sem, ev.val)

        @block.tensor
        def _(e):
            run("pe", e)

        @block.scalar
        def _(e):
            run("act", e)

        @block.vector
        def _(e):
            run("dve", e)

        @block.gpsimd
        def _(e):
            run("pool", e)

        @block.sync
        def _(e):
            run("sp", e, final=True)


def build_nc():
    nc = bass.Bass("TRN2", target_bir_lowering=False)

    def din(name, shape, dt=F32):
        return nc.dram_tensor(name, list(shape), dt, kind="ExternalInput").ap()

    def dout(name, shape, dt=F32):
        return nc.dram_tensor(name, list(shape), dt, kind="ExternalOutput").ap()

    xall = din("xall", [NTILES * 128, D])
    xh = din("xh", [128, D])
    hist = din("hist", [2, 15, 512])
    w_in = din("w_in", [D, 1536])
    w_out = din("w_out", [D, D])
    w_up = din("w_up", [D, 4096])
    w_down = din("w_down", [4096, D])
    pool_w = din("pool_w", [4, 128, 128])
    pool_scale = din("pool_scale", [512])
    gmlp_ws = din("gmlp_ws", [4, 128, 128])
    gmlp_b = din("gmlp_b", [4, 128])
    g1 = din("norm1_g", [D])
    g2 = din("norm2_g", [D])
    gf = din("normf_g", [D])
    ident = din("ident", [128, 128])
    invc = din("invc", [128, 64])
    yall = dout("yall", [NTILES * 128, D])
    spp = dout("spp", [15, 512])
    sps = dout("sps", [2, 15, 512])
    vs = dout("vs", [128, 512])
    scr = nc.dram_tensor("scr", [16, 128, 8, 512], BF16, kind="Internal").ap()

    P = Prog()
    with contextlib.ExitStack() as es:
        def sb(name, shape, dt):
            return es.enter_context(nc.sbuf_tensor(name, list(shape), dt))

        X = [sb(f"X{b}", [128, NT, D], F32) for b in range(NXB)]
        XH = X[1][:, 0, :]
        WSF = X[1][:, 1, 0:512].rearrange("p (g j) -> p g j", j=128)
        WSFS = X[1][:, 1, 512:1024].rearrange("p (g j) -> p g j", j=128)
        HB = [sb(f"HB{j}", [128, D], BF16) for j in range(3)]
        HT = sb("HT", [128, 8, TOK], BF16)
        H2T = sb("H2T", [128, 8, TOK], BF16)
        A = sb("A", [128, 4, AW], F32)
        PQ = [sb(f"PQ{j}", [128, AW], F32) for j in range(2)]
        T16 = sb("T16", [128, 16], F32)
        DT = sb("DT", [128, 4, TOK], BF16)
        UT = sb("UT", [128, 4, TOK], BF16)
        V = sb("V", [128, NT, 512], BF16)
        MIXT = sb("MIXT", [128, 8, TOK], BF16)
        ST = sb("ST", [128, 32, TOK], BF16)
        HTH = ST[:, 0:3, :].rearrange("p a b -> p (a b)")[:, 0:1024].rearrange("p (k t) -> p k t", t=128)
        WSB = ST[:, 3:5, :].rearrange("p a b -> p (a b)")[:, 0:512].rearrange("p (g t) -> p g t", t=128)
        WSBS = ST[:, 5:7, :].rearrange("p a b -> p (a b)")[:, 0:512].rearrange("p (g t) -> p g t", t=128)
        VF = X[2][:, 0, 512:1024]
        WIN = sb("WIN", [128, 8, 1536], BF16)
        WOUT = sb("WOUT", [128, 8, D], BF16)
        RING = [sb(f"RING{s}", [128, 8, 512], BF16) for s in range(NRING)]
        G1B = sb("G1B", [128, D], F32)
        G2B = sb("G2B", [128, D], F32)
        GFB = sb("GFB", [128, D], F32)
        JUNK = sb("JUNK", [128, D], BF16)
        IDF = sb("IDF", [128, 128], F32)
        IDB = sb("IDB", [128, 128], BF16)
        POOLW = sb("POOLW", [128, 4, 128], BF16)
        PSC = sb("PSC", [128, 4], F32)
        INVC = sb("INVC", [128, 64], F32)
        WSMT = sb("WSMT", [128, 4, 128], BF16)
        WSMTS = sb("WSMTS", [128, 4, 128], BF16)
        BB = sb("BB", [128, 4, 128], F32)
        BBS = sb("BBS", [128, 4, 128], F32)
        GT = [sb(f"GT{j}", [128, TOK], F32) for j in range(1)]
        HSA = sb("HSA", [128, 4, 2, 16], F32)
        assert (NS - 1) % NXB != 2 and (NS - 2) % NXB != 2
        SPO = X[2][0:15, :, 0:512]
        SS = {k: sb("SS" + k, [128, 4], F32) for k in ("1", "2", "f", "h")}
        RS = {k: sb("RS" + k, [128, 4], F32) for k in ("1", "2", "f", "h")}
        VAR = {k: sb("VAR" + k, [128, 4], F32) for k in ("1", "2", "f", "h")}
        NWT = {k: sb("NWT" + k, [128, 4], F32) for k in ("1", "2", "f", "h")}
        ps = es.enter_context(nc.psum_tensor("ps", [128, 8, 512], F32))

        def bank(i):
            return ps[:, i, :]

        def bankbf(i):
            return ps[:, i, :].bitcast(BF16)

        front_banks = [5, 6, 7]
        up_banks = [3, 4, 0, 1, 2]
        st = {"front": 0, "up": 0, "hb": 0, "ring": 0, "gt": 0}

        def next_front():
            b = front_banks[st["front"] % len(front_banks)]
            st["front"] += 1
            return b

        def next_up():
            b = up_banks[st["up"] % len(up_banks)]
            st["up"] += 1
            return b

        def next_hb():
            j = st["hb"] % 3
            st["hb"] += 1
            return j

        add = P.add

        add("dve", lambda e: e.memset(WSFS, 0.0), writes=["WSFS"])
        add("dve", lambda e: e.memset(A[:], 0.0), writes=[("A", g) for g in range(4)] + [("Ah", g) for g in range(4)])
        add("dve", lambda e: e.memset(HSA[:], 0.0), writes=["HSA"])
        add("sp", lambda e: e.dma_start(out=IDF[:], in_=ident[:, :]), writes=["IDF"], dma="su0")
        for n in range(NT):
            add("sp", lambda e, n=n: e.dma_start(out=X[0][:, n, :], in_=xall[n * 128:(n + 1) * 128, :]),
                writes=[("X", 0, n)], dma=f"xld0sp_{n}")
        add("sp", lambda e: e.dma_start(out=XH, in_=xh[:, :]), writes=[("X", 1, 0)], dma="su2")
        add("sp", lambda e: e.dma_start(out=G1B[:], in_=g1.partition_broadcast(128)), writes=["G1B"], dma="su1")
        add("sp", lambda e: e.dma_start(out=WSF, in_=gmlp_ws.rearrange("g i j -> i g j")), writes=[("X", 1, 1)], dma="su8")
        add("sp", lambda e: e.dma_start(out=WSFS[0:64, :, 0:64], in_=gmlp_ws[:, 0:64, 0:64].rearrange("g i j -> i g j")),
            writes=["WSFS"], dma="su9")
        add("sp", lambda e: e.dma_start(out=WSFS[64:128, :, 64:128], in_=gmlp_ws[:, 0:64, 0:64].rearrange("g i j -> i g j")),
            reads=["WSFS"], writes=["WSFSb"], dma="su10")
        add("sp", lambda e: e.dma_start(out=BB[:].rearrange("p g c -> p (g c)"),
                                        in_=gmlp_b.rearrange("g c -> (g c)").partition_broadcast(128)),
            writes=["BB"], dma="sb0")
        for half in range(2):
            add("sp", lambda e, half=half: e.dma_start(out=BBS[:, :, half * 64:(half + 1) * 64],
                                                       in_=gmlp_b[:, 0:64].partition_broadcast(128)),
                writes=[("BBS", half)], dma=f"sb1_{half}")
        add("sp", lambda e: e.dma_start(out=PSC[:], in_=pool_scale.rearrange("(g p) -> p g", p=128),
                                        allow_slow_non_contiguous=True), writes=["PSC"], dma="su5")
        add("sp", lambda e: e.dma_start(out=INVC[:], in_=invc[:, :]), writes=["INVC"], dma="su6")
        add("sp", lambda e: e.dma_start(out=G2B[:], in_=g2.partition_broadcast(128)), writes=["G2B"], dma="su3")
        add("sp", lambda e: e.dma_start(out=GFB[:], in_=gf.partition_broadcast(128)), writes=["GFB"], dma="su4")

        def gen_L(i):
            b = i % NXB
            for n in range(NT):
                t = i * NT + n
                add("pool", lambda e, b=b, n=n, t=t: e.dma_start(out=X[b][:, n, :], in_=xall[t * 128:(t + 1) * 128, :]),
                    writes=[("X", b, n)], dma=f"xld{b}_{n}")

        for j in range(3):
            add("pool", lambda e, j=j: e.dma_start(
                out=WIN[:, :, j * 512:(j + 1) * 512],
                in_=w_in[:, j * 512:(j + 1) * 512].rearrange("(k p) c -> p k c", p=128)),
                writes=[("WIN", j)], dma=f"win{j}")
        for j in range(2):
            add("pool", lambda e, j=j: e.dma_start(
                out=WOUT[:, :, j * 512:(j + 1) * 512],
                in_=w_out[:, j * 512:(j + 1) * 512].rearrange("(k p) c -> p k c", p=128)),
                writes=[("WOUT", j)], dma=f"wout{j}")
        add("pool", lambda e: e.dma_start(out=POOLW[:], in_=pool_w.rearrange("g c e -> c g e")),
            writes=["POOLW"], dma="su7")
        def gen_setup_gmlp_T():
            for (src, dst, nm) in ((WSB, WSMT, "WSB"), (WSBS, WSMTS, "WSBS")):
                fb = next_front()

                def trw(e, src=src, fb=fb):
                    for g in range(4):
                        i_ = e.transpose(out=bankbf(fb)[:, g * 128:(g + 1) * 128], in_=src[:, g, :], identity=IDB[:])
                    return i_
                add("pe", trw, reads=[nm, "IDB"], writes=[("BANK", fb)])
                add("act", lambda e, dst=dst, fb=fb: e.activation(
                    out=dst[:], in_=bankbf(fb)[:, 0:512].rearrange("p (g t) -> p g t", t=128), func=AF.Copy),
                    writes=[("BANK", fb), "WSMT" + nm])

        def gen_setup_gmlp():
            add("dve", lambda e: e.tensor_copy(out=WSB, in_=WSF), reads=[("X", 1, 1)], writes=["WSB"])
            add("dve", lambda e: e.memset(WSB[0:64, :, 64:128], 0.0), writes=["WSB"])
            add("dve", lambda e: e.tensor_copy(out=WSBS, in_=WSFS), reads=["WSFS", "WSFSb", ("X", 1, 1)], writes=["WSBS"])

        for s in range(2):
            for g in range(4):
                add("sp", lambda e, s=s, g=g: e.dma_start(
                    out=HSA[:, g, s, 1:16], in_=hist[s, :, g * 128:(g + 1) * 128].rearrange("r c -> c r"),
                    allow_slow_non_contiguous=True),
                    reads=["HSA"], writes=[("HSAw", s, g)], dma=f"hs{s}{g}")
        for k in ("1", "2", "f", "h"):
            add("dve", lambda e, k=k: e.memset(SS[k][:], 0.0), writes=[("SS", k)])

        def norm_memset(kind):
            add("dve", lambda e: e.memset(SS[kind][:], 0.0), writes=[("SS", kind)])

        def norm_square(kind, j, ap, res):
            add("act", lambda e, ap=ap, j=j: e.activation(out=JUNK[:], in_=ap, func=AF.Square,
                                                         accum_out=SS[kind][:, j:j + 1]),
                reads=list(res) + [("SS", kind)], writes=[("SSc", kind, j), "JUNK"])

        def norm_stats(kind, srcs, gb, squares_done=False):
            n = len(srcs)
            if not squares_done:
                norm_memset(kind)
                for j, (ap, res) in enumerate(srcs):
                    norm_square(kind, j, ap, res)
            VR, NW = VAR[kind], NWT[kind]
            add("dve", lambda e: e.tensor_scalar(out=VR[:, 0:n], in0=SS[kind][:, 0:n], scalar1=1.0 / D, scalar2=EPS,
                                                 op0=ALU.mult, op1=ALU.add),
                reads=[("SSc", kind, j) for j in range(n)] + [("SS", kind)], writes=[("VAR", kind)])
            add("act", lambda e: e.activation(out=RS[kind][:, 0:n], in_=VR[:, 0:n], func=AF.Sqrt),
                reads=[("VAR", kind)], writes=[("RS", kind)])
            add("dve", lambda e: e.reciprocal(out=RS[kind][:, 0:n], in_=RS[kind][:, 0:n]), writes=[("RS", kind)])
            add("dve", lambda e: e.tensor_tensor(out=NW[:, 0:n], in0=RS[kind][:, 0:n], in1=RS[kind][:, 0:n], op=ALU.mult),
                reads=[("RS", kind)], writes=[("NWT", kind)])
            add("dve", lambda e: e.tensor_tensor(out=NW[:, 0:n], in0=NW[:, 0:n], in1=VR[:, 0:n], op=ALU.mult),
                reads=[("VAR", kind)], writes=[("NWT", kind)])
            add("dve", lambda e: e.tensor_scalar(out=NW[:, 0:n], in0=NW[:, 0:n], scalar1=-0.5, scalar2=1.5,
                                                 op0=ALU.mult, op1=ALU.add), writes=[("NWT", kind)])
            add("dve", lambda e: e.tensor_tensor(out=RS[kind][:, 0:n], in0=RS[kind][:, 0:n], in1=NW[:, 0:n], op=ALU.mult),
                reads=[("NWT", kind)], writes=[("RS", kind)])

        def norm_apply(kind, j, ap, res, gb, gbres):
            hb = next_hb()
            add("dve", lambda e: e.scalar_tensor_tensor(out=HB[hb][:], in0=ap, scalar=RS[kind][:, j:j + 1], in1=gb[:],
                                                        op0=ALU.mult, op1=ALU.mult),
                reads=list(res) + [("RS", kind), gbres], writes=[("HB", hb)])
            return hb

        def norm_T(hb, dstT, dst_res, ncol0):
            fb = next_front()

            def tr(e):
                for k in range(8):
                    i_ = e.transpose(out=bankbf(fb)[:, k * 128:(k + 1) * 128], in_=HB[hb][:, k * 128:(k + 1) * 128],
                                     identity=IDB[:])
                return i_
            add("pe", tr, reads=[("HB", hb), "IDB"], writes=[("BANK", fb)])
            add("act", lambda e: e.activation(out=dstT[:, :, ncol0:ncol0 + 128],
                                              in_=bankbf(fb)[:, 0:1024].rearrange("p (k t) -> p k t", t=128),
                                              func=AF.Copy),
                writes=[("BANK", fb), dst_res])

        def norm_apply_T(kind, j, ap, res, gb, gbres, dstT, dst_res, ncol0):
            norm_T(norm_apply(kind, j, ap, res, gb, gbres), dstT, dst_res, ncol0)

        WINres = [("WIN", 0), ("WIN", 1), ("WIN", 2)]
        WOUTres = [("WOUT", 0), ("WOUT", 1)]

        def segs(i):
            if i == NS - 1:
                return [(16, 0, 256), (288, 256, 64), (368, 320, 64)]
            return [(16, 0, TOK)]

        def gen_halo():
            srcs = [(X[0][:, n, :], [("X", 0, n)]) for n in range(NT)] + [(XH, [("X", 1, 0)])]
            norm_stats("1", srcs, G1B)
            add("dve", lambda e: e.tensor_copy(out=IDB[:], in_=IDF[:]), reads=["IDF"], writes=["IDB"])
            norm_apply_T("1", NT, XH, [("X", 1, 0)], G1B, "G1B", HTH, "HTH", 0)
            fb = next_front()

            def mm(e):
                for m in range(4):
                    for k in range(8):
                        i_ = e.matmul(bank(fb)[:, m * 128:(m + 1) * 128], lhsT=WIN[:, k, m * 128:(m + 1) * 128],
                                      rhs=HTH[:, k, :], start=(k == 0), stop=(k == 7))
                return i_
            add("pe", mm, reads=["HTH", ("WIN", 0)], writes=[("BANK", fb)])
            add("act", lambda e: e.activation(out=A[:, :, 0:16],
                                              in_=bank(fb)[:, :].rearrange("p (m t) -> p m t", t=128)[:, :, 112:128],
                                              func=AF.Copy),
                writes=[("BANK", fb)] + [("Ah", g) for g in range(4)])

        hbs = {}

        def gen_N1_elem(i, stats=True):
            b = i % NXB
            srcs = [(X[b][:, n, :], [("X", b, n)]) for n in range(NT)]
            if stats:
                norm_stats("1", srcs, G1B)
            hbs[("1", i)] = [norm_apply("1", n, X[b][:, n, :], [("X", b, n)], G1B, "G1B") for n in range(NT)]

        def gen_N1_T(i):
            for n in range(NT):
                norm_T(hbs[("1", i)][n], HT, ("HT", n), n * 128)

        def gen_WI(i):
            sg = segs(i)
            HTall = [("HT", n) for n in range(NT)]
            for m in range(8):
                fb = next_front()

                def mm(e, m=m, fb=fb):
                    for k in range(8):
                        i_ = e.matmul(bank(fb)[:, 0:TOK], lhsT=WIN[:, k, m * 128:(m + 1) * 128], rhs=HT[:, k, :],
                                      start=(k == 0), stop=(k == 7))
                    return i_
                add("pe", mm, reads=HTall + [("WIN", m // 4)], writes=[("BANK", fb)])
                if m < 4:
                    for si, (acol, tok0, ln) in enumerate(sg):
                        add("act", lambda e, m=m, fb=fb, acol=acol, tok0=tok0, ln=ln: e.activation(
                            out=A[:, m, acol:acol + ln], in_=bank(fb)[:, tok0:tok0 + ln], func=AF.Copy),
                            writes=[("BANK", fb), ("A", m)])
                else:
                    add("act", lambda e, m=m, fb=fb: e.activation(out=UT[:, m - 4, :], in_=bank(fb)[:, 0:TOK],
                                                                  func=AF.Gelu_apprx_tanh),
                        writes=[("BANK", fb), ("UT", m - 4)])
            for n in range(NT):
                fb = next_front()
                is_sample = (i * NT + n == NTILES - 1)

                def mm(e, n=n, fb=fb):
                    for k in range(8):
                        i_ = e.matmul(bank(fb)[:, :], lhsT=HT[:, k, n * 128:(n + 1) * 128], rhs=WIN[:, k, 1024:1536],
                                      start=(k == 0), stop=(k == 7))
                    return i_
                add("pe", mm, reads=[("HT", n), ("WIN", 2)], writes=[("BANK", fb)])
                if not is_sample:
                    add("act", lambda e, n=n, fb=fb: e.activation(out=V[:, n, :], in_=bank(fb)[:, :],
                                                                  func=AF.Gelu_apprx_tanh),
                        writes=[("BANK", fb), ("V", n)])
                else:
                    add("act", lambda e, n=n, fb=fb: e.activation(out=VF, in_=bank(fb)[:, :],
                                                                  func=AF.Gelu_apprx_tanh),
                        writes=[("BANK", fb), ("X", 2, 0)])
                    add("dve", lambda e, n=n: e.tensor_copy(out=V[:, n, :], in_=VF), reads=[("X", 2, 0)], writes=[("V", n)])
                    fin.append(add("pool", lambda e: e.dma_start(out=vs[:, :], in_=VF), reads=[("X", 2, 0)], dma="o_vs"))
            if i == NS - 1:
                for si, c0 in enumerate((241, 305, 369)):
                    fb = next_front()

                    def mm(e, c0=c0, fb=fb):
                        for k in range(8):
                            i_ = e.matmul(bank(fb)[0:15, :], lhsT=HT[:, k, c0:c0 + 15], rhs=WIN[:, k, 0:512],
                                          start=(k == 0), stop=(k == 7))
                        return i_
                    add("pe", mm, reads=HTall + [("WIN", 0)], writes=[("BANK", fb)])
                    add("act", lambda e, si=si, fb=fb: e.activation(out=SPO[:, si, :], in_=bank(fb)[0:15, :], func=AF.Copy),
                        writes=[("BANK", fb), ("X", 2, si)])
                fin.append(add("pool", lambda e: e.dma_start(out=spp[:, :], in_=SPO[:, 0, :]), reads=[("X", 2, 0)], dma="o_spp"))
                fin.append(add("pool", lambda e: e.dma_start(out=sps.rearrange("s r c -> r s c"), in_=SPO[:, 1:3, :]),
                               reads=[("X", 2, 1), ("X", 2, 2)], dma="o_sps"))

        def gen_PL(i):
            sg = segs(i)
            W = AW if i == NS - 1 else 16 + TOK
            if i == NS - 1:
                for s in range(2):
                    c0 = 272 + 80 * s
                    add("dve", lambda e, s=s, c0=c0: e.tensor_copy(out=A[:, :, c0:c0 + 16], in_=HSA[:, :, s, :]),
                        reads=["HSA"] + [("HSAw", s, g) for g in range(4)], writes=[("A", g) for g in range(4)])
            for g in range(4):
                src = A[:, g, :]
                src_res = [("A", g), ("Ah", g)]
                cur = None
                sh = 1
                for lvl in range(g + 1):
                    dstj = lvl % 2
                    dst = PQ[dstj]
                    lo = 2 * sh - 1
                    if lvl == 0:
                        i0, i1 = src[:, lo:W], src[:, lo - sh:W - sh]
                        rd = src_res
                    else:
                        i0, i1 = cur[:, lo:W], cur[:, lo - sh:W - sh]
                        rd = [("PQ", 1 - dstj)]
                    add("dve", lambda e, dst=dst, lo=lo, i0=i0, i1=i1: e.tensor_tensor(out=dst[:, lo:W], in0=i0, in1=i1,
                                                                                    op=ALU.add),
                        reads=rd, writes=[("PQ", dstj)])
                    cur = dst
                    curj = dstj
                    sh *= 2
                wg = POOL_W[g]
                for (acol, tok0, ln) in sg:
                    add("dve", lambda e, g=g, cur=cur, acol=acol, tok0=tok0, ln=ln, wg=wg: e.scalar_tensor_tensor(
                        out=DT[:, g, tok0:tok0 + ln], in0=cur[:, acol:acol + ln], scalar=1.0 / wg,
                        in1=A[:, g, acol:acol + ln], op0=ALU.mult, op1=ALU.subtract),
                        reads=[("PQ", curj), ("A", g)], writes=[("DT", g)])
                if i == 0:
                    add("dve", lambda e, g=g, cur=cur: e.tensor_tensor(out=T16[:], in0=cur[:, 16:32],
                                                                       in1=INVC[:, g * 16:(g + 1) * 16], op=ALU.mult),
                        reads=[("PQ", curj), "INVC"], writes=["T16"])
                    add("dve", lambda e, g=g: e.tensor_tensor(out=DT[:, g, 0:16], in0=T16[:], in1=A[:, g, 16:32],
                                                              op=ALU.subtract),
                        reads=["T16", ("A", g)], writes=[("DT", g)])
            if i < NS - 1:
                add("dve", lambda e: e.tensor_copy(out=A[:, :, 0:16], in_=A[:, :, TOK:TOK + 16]),
                    reads=[("A", g) for g in range(4)], writes=[("Ah", g) for g in range(4)])

        def gen_PG_pieces(i):
            pieces = []

            def pool_piece(g):
                fb = next_front()
                add("pe", lambda e, g=g, fb=fb: e.matmul(bank(fb)[:, 0:TOK], lhsT=POOLW[:, g, :], rhs=DT[:, g, :],
                                                        start=True, stop=True),
                    reads=["POOLW", ("DT", g)], writes=[("BANK", fb)])
                add("dve", lambda e, g=g, fb=fb: e.tensor_scalar(out=MIXT[:, g, :], in0=bank(fb)[:, 0:TOK],
                                                                scalar1=PSC[:, g:g + 1], scalar2=None, op0=ALU.mult),
                    reads=["PSC"], writes=[("BANK", fb), ("MIXT", g)])

            def gmlp_piece(g):
                fb = next_front()

                def mm(e, g=g, fb=fb):
                    for n in range(NT):
                        smp = (i * NT + n == NTILES - 1)
                        wm = WSMTS if smp else WSMT
                        i_ = e.matmul(bank(fb)[:, n * 128:(n + 1) * 128], lhsT=V[:, n, g * 128:(g + 1) * 128], rhs=wm[:, g, :],
                                      start=True, stop=True)
                    return i_
                add("pe", mm, reads=[("V", n) for n in range(NT)] + ["WSMTWSB", "WSMTWSBS"], writes=[("BANK", fb)])
                gj = st["gt"] % len(GT)
                st["gt"] += 1
                npr = NT - 1 if i == NS - 1 else NT
                add("dve", lambda e, g=g, fb=fb, gj=gj, npr=npr: e.tensor_tensor(
                    out=GT[gj][:, 0:npr * 128].rearrange("p (n c) -> p n c", c=128),
                    in0=bank(fb)[:, 0:npr * 128].rearrange("p (n c) -> p n c", c=128),
                    in1=BB[:, g, :].unsqueeze(1).to_broadcast([128, npr, 128]), op=ALU.add),
                    reads=["BB"], writes=[("BANK", fb), ("GT", gj)])
                if npr < NT:
                    add("dve", lambda e, g=g, fb=fb, gj=gj, npr=npr: e.tensor_tensor(
                        out=GT[gj][:, npr * 128:TOK], in0=bank(fb)[:, npr * 128:TOK], in1=BBS[:, g, :], op=ALU.add),
                        reads=[("BBS", 0), ("BBS", 1)], writes=[("BANK", fb), ("GT", gj)])
                add("dve", lambda e, g=g, gj=gj: e.tensor_tensor(out=MIXT[:, 4 + g, :], in0=GT[gj][:, :], in1=UT[:, g, :],
                                                                op=ALU.mult),
                    reads=[("UT", g), ("GT", gj)], writes=[("MIXT", 4 + g)])

            for g in range(4):
                pieces.append(lambda g=g: pool_piece(g))
            for g in range(4):
                pieces.append(lambda g=g: gmlp_piece(g))
            return pieces

        def gen_PG(i):
            for p in gen_PG_pieces(i):
                p()

        def gen_WO_pe(i):
            b = i % NXB
            MIXall = [("MIXT", k) for k in range(8)]
            for n in range(NT):
                for hh in range(2):
                    fb = next_front()

                    def mm(e, n=n, hh=hh, fb=fb):
                        for k in range(8):
                            i_ = e.matmul(bank(fb)[:, :], lhsT=MIXT[:, k, n * 128:(n + 1) * 128],
                                          rhs=WOUT[:, k, hh * 512:(hh + 1) * 512], start=(k == 0), stop=(k == 7))
                        return i_
                    add("pe", mm, reads=MIXall + [("WOUT", hh)], writes=[("BANK", fb)])
                    add("dve", lambda e, n=n, hh=hh, fb=fb: e.tensor_tensor(
                        out=X[b][:, n, hh * 512:(hh + 1) * 512], in0=bank(fb)[:, :], in1=X[b][:, n, hh * 512:(hh + 1) * 512],
                        op=ALU.add),
                        writes=[("BANK", fb), ("X", b, n)])

        def gen_N2_elem(i):
            b = i % NXB
            srcs = [(X[b][:, n, :], [("X", b, n)]) for n in range(NT)]
            norm_stats("2", srcs, G2B)
            hbs[("2", i)] = [norm_apply("2", n, X[b][:, n, :], [("X", b, n)], G2B, "G2B") for n in range(NT)]

        def gen_N2_T(i):
            for n in range(NT):
                norm_T(hbs[("2", i)][n], H2T, ("H2T", n), n * 128)

        pending_stores = []

        def flush_stores():
            while pending_stores:
                s_, q_ = pending_stores.pop(0)
                add("sp", lambda e, s_=s_, q_=q_: e.dma_start(out=scr[q_], in_=RING[s_][:]),
                    reads=[("RING", s_)], writes=[("SCR", q_)], dma=f"scrst{s_}")

        def ring_load(i, q):
            seq = i * 16 + q
            s = seq % NRING
            late = (q % 2 == 1)
            cast = (i == 0) or (i == 1 and late)
            store = (i == 0 and not late) or (i == 1 and late)
            if cast:
                if q < 8:
                    src = w_up[:, q * 512:(q + 1) * 512].rearrange("(k p) c -> p k c", p=128)
                else:
                    hh, kg = divmod(q - 8, 4)
                    src = w_down[kg * 1024:(kg + 1) * 1024, hh * 512:(hh + 1) * 512].rearrange("(k p) c -> p k c", p=128)
                add("pool", lambda e, s=s, src=src: e.dma_start(out=RING[s][:], in_=src),
                    writes=[("RING", s)], dma=f"ringc{s}")
                flush_stores()
                if store:
                    pending_stores.append((s, q))
            else:
                flush_stores()
                add("sp", lambda e, s=s, q=q: e.dma_start(out=RING[s][:], in_=scr[q]),
                    reads=[("SCR", q)], writes=[("RING", s)], dma=f"ring{s}")

        def ring_next(i, q, ahead=NRING - 1):
            seq = i * 16 + q + ahead
            i2, q2 = divmod(seq, 16)
            if i2 < NS:
                ring_load(i2, q2)

        def gen_UP(i, m0, m1):
            H2all = [("H2T", n) for n in range(NT)]
            for m in range(m0, m1):
                q = m // 4
                s = (i * 16 + q) % NRING
                ub = (m + 3) % 5

                def mm(e, m=m, s=s, ub=ub):
                    for k in range(8):
                        i_ = e.matmul(bank(ub)[:, 0:TOK], lhsT=RING[s][:, k, (m % 4) * 128:(m % 4 + 1) * 128],
                                      rhs=H2T[:, k, :], start=(k == 0), stop=(k == 7))
                    return i_
                add("pe", mm, reads=H2all + [("RING", s)], writes=[("BANK", ub)])
                add("act", lambda e, m=m, ub=ub: e.activation(out=ST[:, m, :], in_=bank(ub)[:, 0:TOK], func=AF.Relu),
                    writes=[("BANK", ub), ("ST", m)])
                add("dve", lambda e, m=m: e.tensor_tensor(out=ST[:, m, :], in0=ST[:, m, :], in1=ST[:, m, :], op=ALU.mult),
                    writes=[("ST", m)])
                if m % 4 == 3:
                    ring_next(i, q)

        def gen_DN(i, hh, kg, tile_final=False):
            b = i % NXB
            q = 8 + hh * 4 + kg
            s = (i * 16 + q) % NRING
            for n in range(NT):
                def mm(e, n=n, s=s):
                    for kk in range(8):
                        i_ = e.matmul(bank(n)[:, :], lhsT=ST[:, kg * 8 + kk, n * 128:(n + 1) * 128], rhs=RING[s][:, kk, :],
                                      start=(kg == 0 and kk == 0), stop=(kg == 3 and kk == 7))
                    return i_
                add("pe", mm, reads=[("ST", kg * 8 + kk) for kk in range(8)] + [("RING", s)], writes=[("BANK", n)])
                if kg == 3:
                    add("dve", lambda e, n=n: e.tensor_tensor(
                        out=X[b][:, n, hh * 512:(hh + 1) * 512], in0=bank(n)[:, :], in1=X[b][:, n, hh * 512:(hh + 1) * 512],
                        op=ALU.add),
                        writes=[("BANK", n), ("X", b, n)])
                    if tile_final:
                        norm_square("f", n, X[b][:, n, :], [("X", b, n)])
            ring_next(i, q)

        def gen_final(i, tiles=None, squares_done=False):
            b = i % NXB
            tiles = list(range(NT)) if tiles is None else tiles
            srcs = [(X[b][:, n, :], [("X", b, n)]) for n in tiles]
            norm_stats("f", srcs, GFB, squares_done=squares_done)
            for j, n in enumerate(tiles):
                t = i * NT + n
                lastst = (i == NS - 1)
                halves = [(0, D // 2), (D // 2, D)] if (lastst and n == NT - 1) else [(0, D)]
                for hi, (c0, c1) in enumerate(halves):
                    add("dve", lambda e, n=n, j=j, c0=c0, c1=c1: e.scalar_tensor_tensor(
                        out=X[b][:, n, c0:c1], in0=X[b][:, n, c0:c1], scalar=RS["f"][:, j:j + 1], in1=GFB[:, c0:c1],
                        op0=ALU.mult, op1=ALU.mult),
                        reads=[("RS", "f"), "GFB"], writes=[("X", b, n)])
                    ev = add("sp" if lastst else "pool",
                             lambda e, n=n, t=t, c0=c0, c1=c1: e.dma_start(out=yall[t * 128:(t + 1) * 128, c0:c1],
                                                                           in_=X[b][:, n, c0:c1]),
                             reads=[("X", b, n)], dma=(f"ystsp_{n}_{hi}" if lastst else f"yst{b}_{n}"))
                    fin_y[(b, n, hi)] = ev

        fin = []
        fin_y = {}
        gen_halo()
        gen_N1_elem(0, stats=False)
        gen_N1_T(0)
        gen_setup_gmlp()
        gen_WI(0)
        gen_setup_gmlp_T()
        gen_L(1)
        for q in range(NRING - 1):
            ring_load(0, q)
        gen_PL(0)
        gen_PG(0)
        gen_WO_pe(0)
        gen_N2_elem(0)
        gen_N2_T(0)
        gen_N1_elem(1)
        add("dve", lambda e: e.memset(T16[:], 0.0), reads=["HTH", "WSB", "WSBS"],
            writes=["T16"] + [("ST", m) for m in range(7)])
        for i in range(NS):
            nxt = i + 1 < NS
            last = i == NS - 1
            gen_UP(i, 0, 4)
            if i + 2 < NS:
                gen_L(i + 2)
            if nxt:
                gen_N1_T(i + 1)
            gen_UP(i, 4, 8)
            if nxt:
                gen_WI(i + 1)
                gen_PL(i + 1)
            if i == 0:
                gen_UP(i, 8, 24)
                if nxt:
                    gen_PG(i + 1)
                gen_UP(i, 24, 29)
                if nxt:
                    gen_WO_pe(i + 1)
                    gen_N2_elem(i + 1)
                gen_UP(i, 29, 32)
            else:
                gen_UP(i, 8, 18)
                pieces = gen_PG_pieces(i + 1) if nxt else []
                for k in range(8):
                    if pieces:
                        pieces[k]()
                    gen_UP(i, 18 + k, 19 + k)
                gen_UP(i, 26, 30)
                if nxt:
                    gen_WO_pe(i + 1)
                    gen_N2_elem(i + 1)
                gen_UP(i, 30, 32)
            gen_DN(i, 0, 0)
            gen_DN(i, 0, 1)
            gen_DN(i, 0, 2)
            if nxt:
                gen_N2_T(i + 1)
            gen_DN(i, 0, 3)
            gen_DN(i, 1, 0)
            if i + 2 < NS:
                gen_N1_elem(i + 2)
            gen_DN(i, 1, 1)
            gen_DN(i, 1, 2)
            if last:
                norm_memset("f")
            gen_DN(i, 1, 3, tile_final=last)
            gen_final(i, squares_done=last)
        fin.extend(fin_y.values())
        P.emit(nc, es, final_waits=fin)
    return nc


_NC_CACHE = {}


def kernel(x_prompt, x_sample, state_pool, norm1_g, w_in, pool_w, pool_scale, gmlp_ws, gmlp_b,
           w_out, norm2_g, w_up, w_down, normf_g):
    f32 = np.float32
    x_prompt = np.asarray(x_prompt, f32)
    x_sample = np.asarray(x_sample, f32)
    state_pool = np.asarray(state_pool, f32)
    B, S, _ = x_prompt.shape
    half = S // 2
    if "nc" not in _NC_CACHE:
        _NC_CACHE["nc"] = build_nc()
    nc = _NC_CACHE["nc"]
    shared = {
        "w_in": np.ascontiguousarray(np.asarray(w_in, f32)[0]),
        "w_out": np.ascontiguousarray(np.asarray(w_out, f32)[0]),
        "w_up": np.ascontiguousarray(np.asarray(w_up, f32)[0]),
        "w_down": np.ascontiguousarray(np.asarray(w_down, f32)[0]),
        "pool_w": np.ascontiguousarray(np.asarray(pool_w, f32)[0]),
        "pool_scale": np.ascontiguousarray(np.asarray(pool_scale, f32)[0]),
        "gmlp_ws": np.ascontiguousarray(np.asarray(gmlp_ws, f32)[0]),
        "gmlp_b": np.ascontiguousarray(np.asarray(gmlp_b, f32)[0]),
        "norm1_g": np.ascontiguousarray(np.asarray(norm1_g, f32)[0]),
        "norm2_g": np.ascontiguousarray(np.asarray(norm2_g, f32)[0]),
        "normf_g": np.ascontiguousarray(np.asarray(normf_g, f32)),
        "ident": np.eye(128, dtype=f32),
    }
    invc = np.zeros((2, 128, 64), f32)
    for g, w in enumerate(POOL_W):
        for t in range(16):
            invc[0, :, g * 16 + t] = 1.0 / min(t + 1, w)
            invc[1, :, g * 16 + t] = 1.0 / w
    in_maps = []
    for c in range(NCORES):
        b, hf = divmod(c, 2)
        xs = x_sample[2 * c:2 * c + 2].reshape(128, D)
        xall = np.concatenate([x_prompt[b, hf * half:(hf + 1) * half], xs], axis=0)
        if hf == 0:
            xh = np.zeros((128, D), f32)
        else:
            xh = x_prompt[b, half - 128:half]
        m = dict(shared)
        m["xall"] = np.ascontiguousarray(xall)
        m["xh"] = np.ascontiguousarray(xh)
        m["hist"] = np.ascontiguousarray(state_pool[0, 2 * c:2 * c + 2])
        m["invc"] = invc[hf]
        in_maps.append(m)
    res = run_bass_kernel_spmd(nc, in_maps, core_ids=list(range(NCORES)))
    rs = res.results
    y_prompt = np.empty((B, S, D), f32)
    y_sample = np.empty(x_sample.shape, f32)
    spp = np.empty((1, B, 15, 512), f32)
    spsm = np.empty((1, x_sample.shape[0], 15, 512), f32)
    vsm = np.empty((1, x_sample.shape[0], x_sample.shape[1], 512), f32)
    for c in range(NCORES):
        b, hf = divmod(c, 2)
        ya = rs[c]["yall"]
        y_prompt[b, hf * half:(hf + 1) * half] = ya[:half]
        y_sample[2 * c:2 * c + 2] = ya[half:].reshape(2, 64, D)
        if hf == 1:
            spp[0, b] = rs[c]["spp"]
        spsm[0, 2 * c:2 * c + 2] = rs[c]["sps"]
        vsm[0, 2 * c:2 * c + 2] = rs[c]["vs"].reshape(2, 64, 512)
    return (y_prompt, y_sample, spp, spsm, vsm)
```
